# Optimizing a Trainium2 kernel written in Bass

```python
import math
import jax, jax.numpy as jnp
from jax import lax
import numpy as np

D_MODEL = 2048
BATCH = 8
SEQ = 4096
DEPTH = 2

HEAD_DIM = 128
GRID_W = 64
Q_BLOCK = 128
ROPE_THETA = 10000.0
EPS = 1e-6
NEG_INF = -1e30

A_HEADS = 8
A_KV_HEADS = 2
B_CONFIGS = ((128, 1), (512, 4), (2048, 16))
B_OUT_HEADS = 4
B_HEADS = B_OUT_HEADS * len(B_CONFIGS)
C_HEADS = 4
C_Q_RANK = 512
C_KV_RANK = 256
C_NOPE = 128
C_ROPE = 64
C_V = 128
D_FF = 5632

A_Q = A_HEADS * HEAD_DIM
A_KV = A_KV_HEADS * HEAD_DIM
A_IN = A_Q + 2 * A_KV
B_QKV = B_HEADS * HEAD_DIM
B_IN = 3 * B_QKV
C_IN = C_Q_RANK + C_KV_RANK + C_ROPE
W_IN = A_IN + B_IN + C_IN
A_OUT = A_HEADS * HEAD_DIM
B_OUT = B_OUT_HEADS * HEAD_DIM
C_OUT = C_HEADS * C_V
MIX_W = A_OUT + B_OUT + C_OUT

kernel_name = "hybrid_parallel_gqa_dilated_mla_macaron"


def rms_norm(x, g):
    xf = x.astype(jnp.float32)
    y = xf * lax.rsqrt(jnp.mean(xf * xf, axis=-1, keepdims=True) + EPS)
    return (y * g.astype(jnp.float32)).astype(x.dtype)


def rope(x, pos):
    half = x.shape[-1] // 2
    inv = ROPE_THETA ** (-jnp.arange(half, dtype=jnp.float32) / half)
    ang = pos.astype(jnp.float32)[:, None] * inv[None, :]
    cos = jnp.cos(ang).astype(x.dtype)
    sin = jnp.sin(ang).astype(x.dtype)
    x1, x2 = x[..., :half], x[..., half:]
    return jnp.concatenate([x1 * cos - x2 * sin, x1 * sin + x2 * cos], axis=-1)


def axial_rope(x, row, col):
    h = x.shape[-1] // 2
    return jnp.concatenate([rope(x[..., :h], row), rope(x[..., h:], col)], axis=-1)


def split_heads(x, n):
    b, s, _ = x.shape
    return x.reshape(b, s, n, -1).transpose(0, 2, 1, 3)


def merge_heads(x):
    b, n, s, d = x.shape
    return x.transpose(0, 2, 1, 3).reshape(b, s, n * d)


def swiglu(x, w_gu, w_down):
    g, u = jnp.split(x @ w_gu, 2, axis=-1)
    return (jax.nn.silu(g) * u) @ w_down


def block_attention(q, k, v, scale):
    b, hk, g, s, dq = q.shape
    nb = s // Q_BLOCK
    qb = jnp.moveaxis(q.reshape(b, hk, g, nb, Q_BLOCK, dq), 3, 0)

    def one(qi):
        sc = jnp.einsum('bkgqd,bksd->bkgqs', qi, k, preferred_element_type=jnp.float32) * scale
        p = jax.nn.softmax(sc, axis=-1)
        return jnp.einsum('bkgqs,bksd->bkgqd', p.astype(v.dtype), v)

    o = lax.map(one, qb)
    return jnp.moveaxis(o, 0, 3).reshape(b, hk, g, s, v.shape[-1])


def dilated_window_attention(q, k, v, dilation, half_span, slopes):
    b, h, s, dh = q.shape
    n = half_span
    L = s // dilation
    nbq = -(-L // n)
    Lp = nbq * n

    def to_sub(t):
        return jnp.swapaxes(t.reshape(b, h, L, dilation, dh), 2, 3)

    qs = jnp.pad(to_sub(q), ((0, 0), (0, 0), (0, 0), (0, Lp - L), (0, 0)))
    qs = qs.reshape(b, h, dilation, nbq, n, dh)

    def key_blocks(t):
        tp = jnp.pad(to_sub(t), ((0, 0), (0, 0), (0, 0), (n, Lp - L + n), (0, 0)))
        tp = tp.reshape(b, h, dilation, nbq + 2, n, dh)
        return jnp.concatenate([tp[:, :, :, :-2], tp[:, :, :, 1:-1], tp[:, :, :, 2:]], axis=4)

    kb, vb = key_blocks(k), key_blocks(v)
    qi = jnp.arange(n)[:, None]
    kc = jnp.arange(3 * n)[None, :]
    step = kc - n - qi
    kpos = (jnp.arange(nbq)[:, None, None] - 1) * n + kc[None]
    mask = (jnp.abs(step) <= n)[None] & (kpos >= 0) & (kpos < L)
    dist = (jnp.abs(step) * dilation).astype(jnp.float32)
    bias = -slopes.astype(jnp.float32)[:, None, None, None, None] * dist
    sc = jnp.einsum('bhrnqd,bhrnkd->bhrnqk', qs, kb, preferred_element_type=jnp.float32) * (dh ** -0.5) + bias
    sc = jnp.where(mask, sc, NEG_INF)
    lse = jax.nn.logsumexp(sc, axis=-1)
    p = jnp.exp(sc - lse[..., None])
    o = jnp.einsum('bhrnqk,bhrnkd->bhrnqd', p.astype(v.dtype), vb)
    o = jnp.swapaxes(o.reshape(b, h, dilation, Lp, dh)[:, :, :, :L], 2, 3).reshape(b, h, s, dh)
    lse = jnp.swapaxes(lse.reshape(b, h, dilation, Lp)[:, :, :, :L], 2, 3).reshape(b, h, s)
    return o, lse


def token_mix(h, t, row, col, w_in, a_q_norm, a_k_norm, b_q_norm, b_k_norm,
              c_q_a_norm, c_q_up, c_kv_a_norm, c_kv_up, c_q_norm, c_k_norm, out_norm, w_out):
    b, s, _ = h.shape
    z = h @ w_in
    za = z[..., :A_IN]
    zb = z[..., A_IN:A_IN + B_IN]
    zc = z[..., A_IN + B_IN:]

    qa = axial_rope(rms_norm(split_heads(za[..., :A_Q], A_HEADS), a_q_norm), row, col)
    ka = axial_rope(rms_norm(split_heads(za[..., A_Q:A_Q + A_KV], A_KV_HEADS), a_k_norm), row, col)
    va = split_heads(za[..., A_Q + A_KV:], A_KV_HEADS)
    qa = qa.reshape(b, A_KV_HEADS, A_HEADS // A_KV_HEADS, s, HEAD_DIM)
    oa = block_attention(qa, ka, va, HEAD_DIM ** -0.5)
    oa = merge_heads(oa.reshape(b, A_HEADS, s, HEAD_DIM))

    qb = rms_norm(split_heads(zb[..., :B_QKV], B_HEADS), b_q_norm)
    kb = rms_norm(split_heads(zb[..., B_QKV:2 * B_QKV], B_HEADS), b_k_norm)
    vb = split_heads(zb[..., 2 * B_QKV:], B_HEADS)
    slopes = 2.0 ** (-8.0 * jnp.arange(1, B_HEADS + 1, dtype=jnp.float32) / B_HEADS)
    outs, lses = [], []
    for g, (win, dil) in enumerate(B_CONFIGS):
        sl = slice(g * B_OUT_HEADS, (g + 1) * B_OUT_HEADS)
        o, lse = dilated_window_attention(qb[:, sl], kb[:, sl], vb[:, sl], dil, win // (2 * dil), slopes[sl])
        outs.append(o)
        lses.append(lse)
    wts = jax.nn.softmax(jnp.stack(lses), axis=0)
    ob = jnp.sum(wts[..., None] * jnp.stack(outs).astype(jnp.float32), axis=0).astype(h.dtype)
    ob = merge_heads(ob)

    cq = rms_norm(zc[..., :C_Q_RANK], c_q_a_norm) @ c_q_up
    ckv = rms_norm(zc[..., C_Q_RANK:C_Q_RANK + C_KV_RANK], c_kv_a_norm) @ c_kv_up
    k_rope = zc[..., C_Q_RANK + C_KV_RANK:]
    qc = rms_norm(split_heads(cq, C_HEADS), c_q_norm)
    kvc = split_heads(ckv, C_HEADS)
    kr = jnp.broadcast_to(k_rope[:, None], (b, C_HEADS, s, C_ROPE))
    kc = rms_norm(jnp.concatenate([kvc[..., :C_NOPE], kr], axis=-1), c_k_norm)
    vc = kvc[..., C_NOPE:]
    qc = jnp.concatenate([qc[..., :C_NOPE], rope(qc[..., C_NOPE:], t)], axis=-1)
    kc = jnp.concatenate([kc[..., :C_NOPE], rope(kc[..., C_NOPE:], t)], axis=-1)
    oc = block_attention(qc[:, :, None], kc, vc, (C_NOPE + C_ROPE) ** -0.5)
    oc = merge_heads(oc[:, :, 0])

    y = jnp.concatenate([
        rms_norm(oa, out_norm[:A_OUT]),
        rms_norm(ob, out_norm[A_OUT:A_OUT + B_OUT]),
        rms_norm(oc, out_norm[A_OUT + B_OUT:]),
    ], axis=-1)
    return y @ w_out


def setup_inputs(seed: int = 0) -> dict:
    key = jax.random.key(seed)
    ks = jax.random.split(key, 24)
    f32 = jnp.float32
    L = DEPTH

    def w(k, shape, fan_in):
        return jax.random.normal(k, shape, f32) * (fan_in ** -0.5)

    def g(k, shape):
        return 1.0 + 0.02 * jax.random.normal(k, shape, f32)

    return {
        "x": jax.random.normal(ks[0], (BATCH, SEQ, D_MODEL), f32),
        "ffn1_norm": g(ks[1], (L, D_MODEL)),
        "ffn1_w_gu": w(ks[2], (L, D_MODEL, 2 * D_FF), D_MODEL),
        "ffn1_w_down": w(ks[3], (L, D_FF, D_MODEL), D_FF),
        "mix_norm": g(ks[4], (L, D_MODEL)),
        "w_in": w(ks[5], (L, D_MODEL, W_IN), D_MODEL),
        "a_q_norm": g(ks[6], (L, HEAD_DIM)),
        "a_k_norm": g(ks[7], (L, HEAD_DIM)),
        "b_q_norm": g(ks[8], (L, HEAD_DIM)),
        "b_k_norm": g(ks[9], (L, HEAD_DIM)),
        "c_q_a_norm": g(ks[10], (L, C_Q_RANK)),
        "c_q_up": w(ks[11], (L, C_Q_RANK, C_HEADS * (C_NOPE + C_ROPE)), C_Q_RANK),
        "c_kv_a_norm": g(ks[12], (L, C_KV_RANK)),
        "c_kv_up": w(ks[13], (L, C_KV_RANK, C_HEADS * (C_NOPE + C_V)), C_KV_RANK),
        "c_q_norm": g(ks[14], (L, C_NOPE + C_ROPE)),
        "c_k_norm": g(ks[15], (L, C_NOPE + C_ROPE)),
        "out_norm": g(ks[16], (L, MIX_W)),
        "w_out": w(ks[17], (L, MIX_W, D_MODEL), MIX_W),
        "ffn2_norm": g(ks[18], (L, D_MODEL)),
        "ffn2_w_gu": w(ks[19], (L, D_MODEL, 2 * D_FF), D_MODEL),
        "ffn2_w_down": w(ks[20], (L, D_FF, D_MODEL), D_FF),
    }


def reference(x, ffn1_norm, ffn1_w_gu, ffn1_w_down, mix_norm, w_in, a_q_norm, a_k_norm,
              b_q_norm, b_k_norm, c_q_a_norm, c_q_up, c_kv_a_norm, c_kv_up, c_q_norm, c_k_norm,
              out_norm, w_out, ffn2_norm, ffn2_w_gu, ffn2_w_down):
    s = x.shape[1]
    rows = s // GRID_W
    t = jnp.arange(s, dtype=jnp.int32)
    row = jnp.repeat(jnp.arange(rows, dtype=jnp.int32), GRID_W)
    col = jnp.tile(jnp.arange(GRID_W, dtype=jnp.int32), rows)
    for l in range(DEPTH):
        x = x + 0.5 * swiglu(rms_norm(x, ffn1_norm[l]), ffn1_w_gu[l], ffn1_w_down[l])
        x = x + token_mix(rms_norm(x, mix_norm[l]), t, row, col, w_in[l], a_q_norm[l], a_k_norm[l],
                          b_q_norm[l], b_k_norm[l], c_q_a_norm[l], c_q_up[l], c_kv_a_norm[l], c_kv_up[l],
                          c_q_norm[l], c_k_norm[l], out_norm[l], w_out[l])
        x = x + 0.5 * swiglu(rms_norm(x, ffn2_norm[l]), ffn2_w_gu[l], ffn2_w_down[l])
    return x
```

```python
import math
from contextlib import ExitStack

import numpy as np
import ml_dtypes

import concourse.bass as bass
import concourse.mybir as mybir
from concourse.bass_utils import run_bass_kernel_spmd

F32 = mybir.dt.float32
BF16 = mybir.dt.bfloat16
AF = mybir.ActivationFunctionType
ALU = mybir.AluOpType

D = 2048
DFF = 5632
NJ = DFF // 128
WIN = 6976
EPS = 1e-6
GL = 78
NCORES = 8

A_Q0, A_K0, A_V0 = 0, 1024, 1280
B_Q0, B_K0, B_V0 = 1536, 3072, 4608
C_Q0, C_KV0, C_R0 = 6144, 6656, 6912
B_CONFIGS = ((128, 1), (512, 4), (2048, 16))


class SemC:
    def __init__(self, h, k):
        self.h = h
        self.k = k
        self.cnt = 0


class Buf:
    def __init__(self, t=None, sem=None, dram=False, name=""):
        self.t = t
        self.sem = sem
        self.w = {}
        self.r = {}
        self.dram = dram
        self.name = name


class Eng:
    def __init__(self, h, semc=None, is_pe=False):
        self.h = h
        self.semc = semc
        self.seen = {}
        self.is_pe = is_pe


class Ring:
    def __init__(self, bufs):
        self.bufs = bufs
        self.i = 0

    def next(self):
        b = self.bufs[self.i % len(self.bufs)]
        self.i += 1
        return b


class KB:
    def __init__(self, S=4096, L=2, dbg=False, phases=None, nranks=8, wnames=None):
        self.nranks = nranks
        self.wnames = wnames if wnames is not None else list(WEIGHT_NAMES)
        self.S = S
        self.L = L
        self.dbg = dbg
        self.phases = phases
        nc = bass.Bass("TRN2", target_bir_lowering=False)
        self.nc = nc
        self.nsem = 0
        self.free_sems = []
        self.free_sems_k = {}
        self.all_sems = []
        self.pe = Eng(nc.tensor, self.new_sem("pe"), is_pe=True)
        self.act = Eng(nc.scalar, self.new_sem("act"))
        self.dve = Eng(nc.vector, self.new_sem("dve"))
        self.pool = Eng(nc.gpsimd, self.new_sem("pool"))
        self.sp = Eng(nc.sync, None)
        self.gq = self.pool
        self.engs = [self.pe, self.act, self.dve, self.pool, self.sp]
        self.dram_bufs = {}

    def new_sem(self, name):
        h = self.nc.alloc_semaphore(f"s_{name}_{self.nsem}")
        sc = SemC(h, self.nsem)
        self.nsem += 1
        self.all_sems.append(sc)
        return sc

    def get_sem(self, kind):
        fl = self.free_sems_k.setdefault(kind, [])
        if fl:
            return fl.pop()
        return self.new_sem("b" + kind)

    def sb(self, es, name, shape, dtype, dma=False):
        self.uid = getattr(self, "uid", 0) + 1
        t = es.enter_context(self.nc.sbuf_tensor(f"{name}_{self.uid}", shape, dtype))
        kind = "g" if dma == "g" else "s"
        sem = self.get_sem(kind) if dma else None
        b = Buf(t, sem, name=name)
        if dma:
            es.callback(self.free_sems_k[kind].append, sem)
        return b

    def dbuf(self, key):
        if key not in self.dram_bufs:
            self.dram_bufs[key] = Buf(dram=True, name=str(key))
        return self.dram_bufs[key]

    def _wait(self, eng, evs):
        for k, (sc, v) in evs.items():
            if eng.is_pe and sc is eng.semc:
                continue
            if eng.seen.get(k, 0) >= v:
                continue
            eng.h.wait_ge(sc.h, v)
            eng.seen[k] = v

    def op(self, eng, fn, reads=(), writes=(), signal=True):
        for b in reads:
            self._wait(eng, b.w)
        for b in writes:
            self._wait(eng, b.w)
            self._wait(eng, b.r)
        ins = fn(eng.h)
        sc = eng.semc
        if signal:
            sc.cnt += 1
            ins.then_inc(sc.h, 1)
            v = sc.cnt
        else:
            v = sc.cnt + 1
        ev = (sc, v)
        for b in reads:
            b.r[sc.k] = ev
        for b in writes:
            if b.dram:
                b.w[sc.k] = ev
            else:
                b.w = {sc.k: ev}
                b.r = {}
        return ins

    def dma(self, q, pairs, reads=(), writes=(), own=None):
        for b in reads:
            self._wait(q, b.w)
        for b in writes:
            self._wait(q, b.r)
            if not b.dram:
                self._wait(q, b.w)
        sc = own.sem
        for (o, i) in pairs:
            sc.cnt += 16
            q.h.dma_start(out=o, in_=i).then_inc(sc.h, 16)
        ev = (sc, sc.cnt)
        for b in reads:
            b.r[sc.k] = ev
        for b in writes:
            if b.dram:
                b.w[sc.k] = ev
            else:
                b.w = {sc.k: ev}
                b.r = {}

    def barrier(self):
        evs = {sc.k: (sc, sc.cnt) for sc in self.all_sems if sc.cnt > 0 and not getattr(sc, 'nobar', False)}
        for e in self.engs:
            self._wait(e, evs)

    def psb(self):
        return self.psring.next()

    def mm(self, out_b, out_ap, lhsT_b, lhsT_ap, rhs_b, rhs_ap, start, stop, signal, **kw):
        return self.op(
            self.pe,
            lambda h: h.matmul(out_ap, lhsT=lhsT_ap, rhs=rhs_ap, start=start, stop=stop, **kw),
            reads=[lhsT_b, rhs_b], writes=[out_b], signal=signal)

    def build(self):
        nc, S, L = self.nc, self.S, self.L
        kin = "ExternalInput"
        dt = nc.dram_tensor
        self.x = dt("x", [S, D], F32, kind=kin).ap()
        self.out = dt("out", [S, D], F32, kind="ExternalOutput").ap()
        R = self.nranks
        self.WSHAPES = {"ffn1_w_gu": (D, 2 * DFF), "ffn1_w_down": (DFF, D), "w_in": (D, WIN), "c_q_up": (512, 768),
                        "c_kv_up": (256, 1024), "w_out": (D, D), "ffn2_w_gu": (D, 2 * DFF), "ffn2_w_down": (DFF, D)}
        self.Wsrc, self.Wsh, self.Wf = {}, {}, {}
        for nm in self.wnames:
            r, c = self.WSHAPES[nm]
            self.Wsrc[nm] = dt(nm, [L, r // R, c], F32, kind=kin).ap()
            for l in range(L):
                self.Wf[nm, l] = dt(f"wf_{nm}_{l}", [r, c], BF16, kind="Internal").ap()
                if R > 1:
                    self.Wsh[nm, l] = dt(f"wsh_{nm}_{l}", [r // R, c], BF16, kind="Internal").ap()
        self.gains_d = dt("gains", [128, 2 * GL], F32, kind=kin).ap()
        self.ident_d = dt("ident", [128, 128], F32, kind=kin).ap()
        self.rot_d = dt("rotm", [128, 192], BF16, kind=kin).ap()
        self.ropeA_d = dt("ropeA", [2, 128, 4096], F32, kind=kin).ap()
        self.ropeC_d = dt("ropeC", [2, 64, 4096], F32, kind=kin).ap()
        self.alibi_d = dt("alibiE", [128, 12, 256], F32, kind=kin).ap()
        sk = "ExternalOutput" if self.dbg else "Internal"
        self.xT = dt("xT", [D, S], F32, kind=sk).ap()
        self.oT = dt("oT", [D, S], F32, kind=sk).ap()
        self.qA = dt("qA", [8, 128, S], BF16, kind=sk).ap()
        self.kA = dt("kA", [2, 128, S], BF16, kind=sk).ap()
        self.vA = dt("vA", [S, 256], BF16, kind=sk).ap()
        self.qB = dt("qB", [12, 128, S], BF16, kind=sk).ap()
        self.kB = dt("kB", [12, 128, S], BF16, kind=sk).ap()
        self.vB = dt("vB", [S, 1536], BF16, kind=sk).ap()
        self.qCn = dt("qCn", [4, 128, S], BF16, kind=sk).ap()
        self.qCr = dt("qCr", [4, 64, S], BF16, kind=sk).ap()
        self.kCn = dt("kCn", [4, 128, S], BF16, kind=sk).ap()
        self.kCr = dt("kCr", [4, 64, S], BF16, kind=sk).ap()
        self.vC = dt("vC", [S, 512], BF16, kind=sk).ap()
        self.xTv = self.xT.rearrange("(c p) t -> p c t", p=128)
        self.oTv = self.oT.rearrange("(c p) t -> p c t", p=128)

        with ExitStack() as es0, nc.allow_low_precision("bf16 matmul operands, fp32 accumulation"), \
                nc.allow_non_contiguous_dma("strided scratch layouts"):
            self.psring = Ring([Buf(es0.enter_context(nc.psum_tensor(f"ps{i}", [128, 512], F32)), name=f"ps{i}")
                                for i in range(8)])
            self.ones = self.sb(es0, "ones", [128, 128], BF16)
            self.ident = self.sb(es0, "identf", [128, 128], F32, dma=True)
            self.gains = self.sb(es0, "gainsb", [128, 2 * GL], F32, dma=True)
            self.rot = self.sb(es0, "rotb", [128, 192], BF16, dma=True)
            self.epsb = self.sb(es0, "epsb", [128, 1], F32)
            self.op(self.dve, lambda h: h.memset(self.ones.t[:, :], 1.0), writes=[self.ones])
            self.op(self.dve, lambda h: h.memset(self.epsb.t[:, :], EPS), writes=[self.epsb])
            self.dma(self.sp, [(self.ident.t[:, :], self.ident_d[:, :])], writes=[self.ident], own=self.ident)
            self.dma(self.sp, [(self.rot.t[:, :], self.rot_d[:, :])], writes=[self.rot], own=self.rot)
            self.dma(self.sp, [(self.gains.t[:, :], self.gains_d[:, :])], writes=[self.gains], own=self.gains)
            self.phase_wprep()
            ph = self.phases
            if ph is None or "tin" in ph:
                self.phase_tin()
                self.barrier()
            for l in range(L):
                if ph is None or f"ffn1_{l}" in ph:
                    self.phase_ffn(l, 0)
                    self.barrier()
                if ph is None or f"m1_{l}" in ph:
                    self.phase_m1(l)
                    self.barrier()
                if ph is None or f"m2a_{l}" in ph:
                    self.phase_m2_dense(l, "A")
                    self.barrier()
                if ph is None or f"m2c_{l}" in ph:
                    self.phase_m2_dense(l, "C")
                    self.barrier()
                if ph is None or f"m2b_{l}" in ph:
                    self.phase_m2_b(l)
                    self.barrier()
                if ph is None or f"m3_{l}" in ph:
                    self.phase_m3(l)
                    self.barrier()
                if ph is None or f"ffn2_{l}" in ph:
                    self.phase_ffn(l, 1)
                    self.barrier()
            if ph is None or "tout" in ph:
                self.phase_tout()
            self.barrier()
        return nc

    def phase_wprep(self):
        L, R = self.L, self.nranks
        order = [(nm, l) for l in range(L) for nm in self.wnames]
        csem = self.new_sem("cast")
        ccsem = self.new_sem("cc")
        ccsem.nobar = True
        for (nm, l) in order:
            dst = self.Wsh[nm, l] if R > 1 else self.Wf[nm, l]
            nrow = dst.shape[0]
            for r0 in range(0, nrow, 128):
                r1 = min(nrow, r0 + 128)
                csem.cnt += 16
                self.pool.h.dma_start(out=dst[r0:r1, :], in_=self.Wsrc[nm][l, r0:r1, :]).then_inc(csem.h, 16)
            if R == 1:
                self.dbuf(("W", nm, l)).w[csem.k] = (csem, csem.cnt)
        if R > 1:
            self._wait(self.pool, {csem.k: (csem, csem.cnt)})
            for (nm, l) in order:
                ccsem.cnt += 1
                self.pool.h.collective_compute("AllGather", ALU.bypass, replica_groups=[list(range(R))],
                                               ins=[self.Wsh[nm, l][:, :]], outs=[self.Wf[nm, l][:, :]]
                                               ).then_inc(ccsem.h)
                self.dbuf(("W", nm, l)).w[ccsem.k] = (ccsem, ccsem.cnt)

    def phase_tin(self):
        S = self.S
        with ExitStack() as es:
            xin = Ring([self.sb(es, f"xin{i}", [128, D], F32, dma=True) for i in range(2)])
            stg = Ring([self.sb(es, f"stg{i}", [128, 16, 512], F32, dma=True) for i in range(2)])
            for blk in range(S // 512):
                st = stg.next()
                for s4 in range(4):
                    ts = blk * 4 + s4
                    xi = xin.next()
                    self.dma(self.sp, [(xi.t[:, :], self.x[ts * 128:(ts + 1) * 128, :])], writes=[xi], own=xi)
                    for cg in range(4):
                        pb = self.psb()
                        for cc in range(4):
                            c = cg * 4 + cc
                            self.op(self.pe, lambda h: h.transpose(pb.t[:, cc * 128:(cc + 1) * 128],
                                                                   xi.t[:, c * 128:(c + 1) * 128], self.ident.t[:, :]),
                                    reads=[xi, self.ident], writes=[pb], signal=(cc == 3))
                        o_ap = st.t[:, cg * 4:(cg + 1) * 4, s4 * 128:(s4 + 1) * 128]
                        i_ap = pb.t[:, :].rearrange("p (c t) -> p c t", c=4)
                        if cg % 2 == 0:
                            self.op(self.act, lambda h: h.activation(out=o_ap, in_=i_ap, func=AF.Copy),
                                    reads=[pb], writes=[st])
                        else:
                            self.op(self.dve, lambda h: h.tensor_copy(out=o_ap, in_=i_ap), reads=[pb], writes=[st])
                wb = [self.dbuf(("xT", c, (blk * 512) // 1024)) for c in range(16)]
                self.dma(self.sp, [(self.xTv[:, :, blk * 512:(blk + 1) * 512], st.t[:, :, :])],
                         reads=[st], writes=wb, own=st)

    def phase_tout(self):
        S = self.S
        with ExitStack() as es:
            xin = Ring([self.sb(es, f"xo{i}", [128, 16, 512], F32, dma=True) for i in range(2)])
            stg = Ring([self.sb(es, f"so{i}", [128, D], F32, dma=True) for i in range(2)])
            for blk in range(S // 512):
                xi = xin.next()
                rb = [self.dbuf(("xT", c, (blk * 512) // 1024)) for c in range(16)]
                self.dma(self.sp, [(xi.t[:, :, :], self.xTv[:, :, blk * 512:(blk + 1) * 512])],
                         reads=rb, writes=[xi], own=xi)
                for s4 in range(4):
                    ts = blk * 4 + s4
                    st = stg.next()
                    for cg in range(4):
                        pb = self.psb()
                        for cc in range(4):
                            c = cg * 4 + cc
                            self.op(self.pe, lambda h: h.transpose(pb.t[:, cc * 128:(cc + 1) * 128],
                                                                   xi.t[:, c, s4 * 128:(s4 + 1) * 128],
                                                                   self.ident.t[:, :]),
                                    reads=[xi, self.ident], writes=[pb], signal=(cc == 3))
                        o_ap = st.t[:, cg * 512:(cg + 1) * 512]
                        if cg % 2 == 0:
                            self.op(self.act, lambda h: h.activation(out=o_ap, in_=pb.t[:, :], func=AF.Copy),
                                    reads=[pb], writes=[st])
                        else:
                            self.op(self.dve, lambda h: h.tensor_copy(out=o_ap, in_=pb.t[:, :]),
                                    reads=[pb], writes=[st])
                    self.dma(self.sp, [(self.out[ts * 128:(ts + 1) * 128, :], st.t[:, :])],
                             reads=[st], writes=[self.dbuf("out")], own=st)

    def rstd_from_stat(self, stat_b, out_b, out_ap, dim, n=512, parts=128):
        self.op(self.act, lambda h: h.activation(out=out_ap, in_=stat_b.t[0:parts, 0:n], func=AF.Sqrt,
                                                 bias=self.epsb.t[0:parts, 0:1], scale=1.0 / dim),
                reads=[stat_b, self.epsb], writes=[out_b])
        self.op(self.dve, lambda h: h.reciprocal(out=out_ap, in_=out_ap), reads=[out_b], writes=[out_b])

    def norm_tile(self, src_v, src_key, t0, T, groups, gcol, dims, hT, xring, sqring, rstd):
        ntb = T // 512
        tile_i = t0 // 1024
        stat = {}
        for gi in range(len(groups)):
            for tb in range(ntb):
                stat[gi, tb] = self.psb()
        for gi, g in enumerate(groups):
            for ci, c in enumerate(g):
                xs = xring.next()
                self.dma(self.sp, [(xs.t[:, 0:T], src_v[:, c, t0:t0 + T])],
                         reads=[self.dbuf((src_key, c, tile_i))], writes=[xs], own=xs)
                sq = sqring.next()
                self.op(self.act, lambda h: h.activation(out=sq.t[:, 0:T], in_=xs.t[:, 0:T], func=AF.Square),
                        reads=[xs], writes=[sq])
                for tb in range(ntb):
                    self.mm(stat[gi, tb], stat[gi, tb].t[:, :], self.ones, self.ones.t[:, :],
                            sq, sq.t[:, tb * 512:(tb + 1) * 512],
                            start=(ci == 0), stop=(ci == len(g) - 1), signal=(tb == ntb - 1))
        for gi in range(len(groups)):
            for tb in range(ntb):
                self.rstd_from_stat(stat[gi, tb], rstd[gi], rstd[gi].t[:, tb * 512:(tb + 1) * 512], dims[gi])
        for gi, g in enumerate(groups):
            for c in g:
                xs = xring.next()
                self.dma(self.sp, [(xs.t[:, 0:T], src_v[:, c, t0:t0 + T])],
                         reads=[self.dbuf((src_key, c, tile_i))], writes=[xs], own=xs)
                self.op(self.dve, lambda h: h.scalar_tensor_tensor(
                    out=hT[c].t[:, 0:T], in0=xs.t[:, 0:T], scalar=self.gains.t[:, gcol + c:gcol + c + 1],
                    in1=rstd[gi].t[:, 0:T], op0=ALU.mult, op1=ALU.mult),
                    reads=[xs, self.gains, rstd[gi]], writes=[hT[c]])

    def proj_resid(self, wv, wbuf, nk, kgrp, rhs, t0, T, scale, wdring, xring, oring):
        ntb = T // 512
        tile_i = t0 // 1024
        for cp in range(8):
            banks = {(cc, tb): self.psb() for cc in range(2) for tb in range(ntb)}
            for kq in range(nk // kgrp):
                wd = wdring.next()
                self.dma(self.gq, [(wd.t[:, 0:kgrp, :], wv[:, kq * kgrp:(kq + 1) * kgrp, cp * 256:(cp + 1) * 256])],
                         reads=[wbuf], writes=[wd], own=wd)
                for k1 in range(kgrp):
                    k = kq * kgrp + k1
                    for cc in range(2):
                        for tb in range(ntb):
                            last = (k1 == kgrp - 1 and cc == 1 and tb == ntb - 1)
                            self.mm(banks[cc, tb], banks[cc, tb].t[:, :], wd, wd.t[:, k1, cc * 128:(cc + 1) * 128],
                                    rhs[k], rhs[k].t[:, tb * 512:(tb + 1) * 512],
                                    start=(k == 0), stop=(k == nk - 1), signal=last)
            for cc in range(2):
                c = cp * 2 + cc
                xb = self.dbuf(("xT", c, tile_i))
                xs = xring.next()
                self.dma(self.sp, [(xs.t[:, 0:T], self.xTv[:, c, t0:t0 + T])], reads=[xb], writes=[xs], own=xs)
                for tb in range(ntb):
                    o = oring.next()
                    self.op(self.dve, lambda h: h.scalar_tensor_tensor(
                        out=o.t[:, :], in0=banks[cc, tb].t[:, :], scalar=float(scale),
                        in1=xs.t[:, tb * 512:(tb + 1) * 512], op0=ALU.mult, op1=ALU.add),
                        reads=[banks[cc, tb], xs], writes=[o])
                    self.dma(self.sp, [(self.xTv[:, c, t0 + tb * 512:t0 + (tb + 1) * 512], o.t[:, :])],
                             reads=[o], writes=[xb], own=o)

    def phase_ffn(self, l, which):
        S = self.S
        T = min(1024, S)
        ntb = T // 512
        gcol = l * GL + (0 if which == 0 else 48)
        ngu, ndn = ("ffn1_w_gu", "ffn1_w_down") if which == 0 else ("ffn2_w_gu", "ffn2_w_down")
        wgu_v = self.Wf[ngu, l].rearrange("(kc p) n -> p kc n", p=128)
        wd_v = self.Wf[ndn, l].rearrange("(j p) n -> p j n", p=128)
        wgu_b, wd_b = self.dbuf(("W", ngu, l)), self.dbuf(("W", ndn, l))
        with ExitStack() as es:
            xring = Ring([self.sb(es, f"fx{i}", [128, T], F32, dma=True) for i in range(3)])
            sqring = Ring([self.sb(es, f"fsq{i}", [128, T], BF16) for i in range(2)])
            rstd = [self.sb(es, "frstd", [128, T], F32)]
            hT = [self.sb(es, f"fh{c}", [128, T], BF16) for c in range(16)]
            a = [self.sb(es, f"fa{j}", [128, T], BF16) for j in range(NJ)]
            wring = Ring([self.sb(es, f"fw{i}", [128, 2, 16, 256], BF16, dma="g") for i in range(2)])
            sgring = Ring([self.sb(es, f"fsg{i}", [128, 512], F32) for i in range(2)])
            wdring = Ring([self.sb(es, f"fwd{i}", [128, 11, 256], BF16, dma="g") for i in range(3)])
            oring = Ring([self.sb(es, f"fo{i}", [128, 512], F32, dma=True) for i in range(3)])
            for t0 in range(0, S, T):
                self.norm_tile(self.xTv, "xT", t0, T, [list(range(16))], gcol, [2048], hT, xring, sqring, rstd)
                for jg in range(NJ // 2):
                    wt = wring.next()
                    self.dma(self.gq, [(wt.t[:, 0, :, :], wgu_v[:, :, jg * 256:(jg + 1) * 256]),
                                       (wt.t[:, 1, :, :], wgu_v[:, :, DFF + jg * 256:DFF + (jg + 1) * 256])],
                             reads=[wgu_b], writes=[wt], own=wt)
                    for jj in range(2):
                        j = jg * 2 + jj
                        banks = {(gu, tb): self.psb() for gu in range(2) for tb in range(ntb)}
                        for kc in range(16):
                            for gu in range(2):
                                for tb in range(ntb):
                                    last = (kc == 15 and gu == 1 and tb == ntb - 1)
                                    self.mm(banks[gu, tb], banks[gu, tb].t[:, :], wt,
                                            wt.t[:, gu, kc, jj * 128:(jj + 1) * 128],
                                            hT[kc], hT[kc].t[:, tb * 512:(tb + 1) * 512],
                                            start=(kc == 0), stop=(kc == 15), signal=last)
                        for tb in range(ntb):
                            sg = sgring.next()
                            self.op(self.act, lambda h: h.activation(out=sg.t[:, :], in_=banks[0, tb].t[:, :],
                                                                     func=AF.Silu),
                                    reads=[banks[0, tb]], writes=[sg])
                            self.op(self.dve, lambda h: h.tensor_tensor(
                                out=a[j].t[:, tb * 512:(tb + 1) * 512], in0=banks[1, tb].t[:, :], in1=sg.t[:, :],
                                op=ALU.mult), reads=[banks[1, tb], sg], writes=[a[j]])
                self.proj_resid(wd_v, wd_b, NJ, 11, a, t0, T, 0.5, wdring, xring, oring)

    def headnorm(self, parts, dim, sqr, statb, rs):
        for i, (b, ap, P) in enumerate(parts):
            sq = sqr.next()
            self.op(self.act, lambda h: h.activation(out=sq.t[0:P, :], in_=ap, func=AF.Square), reads=[b], writes=[sq])
            self.mm(statb, statb.t[:, :], self.ones, self.ones.t[0:P, :], sq, sq.t[0:P, :],
                    start=(i == 0), stop=(i == len(parts) - 1), signal=True)
        self.rstd_from_stat(statb, rs, rs.t[:, :], dim)
        return rs

    def scale_part(self, src_b, src_ap, P, rs, gcol, out_b, out_ap):
        self.op(self.dve, lambda h: h.scalar_tensor_tensor(
            out=out_ap, in0=src_ap, scalar=self.gains.t[0:P, gcol:gcol + 1], in1=rs.t[0:P, :],
            op0=ALU.mult, op1=ALU.mult), reads=[src_b, self.gains, rs], writes=[out_b])

    def rope_part(self, qn, P, tab, tcols, rot_ap, R):
        qb = R["qb"].next()
        self.op(self.act, lambda h: h.activation(out=qb.t[0:P, :], in_=qn.t[0:P, :], func=AF.Copy),
                reads=[qn], writes=[qb])
        rb = self.psb()
        self.mm(rb, rb.t[0:P, :], self.rot, rot_ap, qb, qb.t[0:P, :], start=True, stop=True, signal=True)
        t1 = R["t1"].next()
        self.op(self.dve, lambda h: h.tensor_tensor(out=t1.t[0:P, :], in0=qn.t[0:P, :], in1=tab.t[0:P, 0, tcols],
                                                    op=ALU.mult), reads=[qn, tab], writes=[t1])
        t2 = R["t2"].next()
        self.op(self.dve, lambda h: h.tensor_tensor(out=t2.t[0:P, :], in0=rb.t[0:P, :], in1=tab.t[0:P, 1, tcols],
                                                    op=ALU.mult), reads=[rb, tab], writes=[t2])
        ob = R["ob"].next()
        self.op(self.dve, lambda h: h.tensor_tensor(out=ob.t[0:P, :], in0=t1.t[0:P, :], in1=t2.t[0:P, :],
                                                    op=ALU.add), reads=[t1, t2], writes=[ob])
        return ob

    def store_fm(self, ob, P, dst_ap, key):
        self.dma(self.sp, [(dst_ap, ob.t[0:P, :])], reads=[ob], writes=[self.dbuf(key)], own=ob)

    def phase_m1(self, l):
        S = self.S
        T = 1024
        ntb = 2
        gb = l * GL
        win_v = self.Wf["w_in", l].rearrange("(kc p) n -> p kc n", p=128)
        win_b = self.dbuf(("W", "w_in", l))
        with ExitStack() as es:
            sb = lambda n, sh, dt_, dma=False: self.sb(es, n, sh, dt_, dma)
            xring = Ring([sb(f"mx{i}", [128, T], F32, True) for i in range(3)])
            sqring = Ring([sb(f"msq{i}", [128, T], BF16) for i in range(2)])
            rstd = [sb("mrstd", [128, T], F32)]
            hT = [sb(f"mh{c}", [128, T], BF16) for c in range(16)]
            wring = Ring([sb(f"mw{i}", [128, 16, 256], BF16, "g") for i in range(3)])
            tabA = sb("tabA", [128, 2, T], F32, True)
            tabC = sb("tabC", [64, 2, T], F32, True)
            R = {"sq": Ring([sb(f"esq{i}", [128, 512], BF16) for i in range(3)]),
                 "rs": Ring([sb(f"ers{i}", [128, 512], F32) for i in range(3)]),
                 "qn": Ring([sb(f"eqn{i}", [128, 512], F32) for i in range(3)]),
                 "qb": Ring([sb(f"eqb{i}", [128, 512], BF16) for i in range(2)]),
                 "t1": Ring([sb(f"et1{i}", [128, 512], F32) for i in range(2)]),
                 "t2": Ring([sb(f"et2{i}", [128, 512], F32) for i in range(2)]),
                 "ob": Ring([sb(f"eob{i}", [128, 512], BF16, True) for i in range(4)])}
            zq = [sb(f"zq{i}", [128, T], F32) for i in range(4)]
            zkv = [sb(f"zkv{i}", [128, T], F32) for i in range(2)]
            kr = sb("zkr", [64, T], F32)
            cqn = [sb(f"cqn{i}", [128, T], BF16) for i in range(4)]
            ckvn = [sb(f"ckvn{i}", [128, T], BF16) for i in range(2)]
            wcq = sb("wcq", [128, 4, 768], BF16, "g")
            wckv = sb("wckv", [128, 2, 1024], BF16, "g")
            vst = Ring([sb(f"vst{i}", [128, 512], BF16, True) for i in range(3)])
            self.dma(self.gq, [(wcq.t[:, :, :], self.Wf["c_q_up", l].rearrange("(kc p) n -> p kc n", p=128))],
                     reads=[self.dbuf(("W", "c_q_up", l))], writes=[wcq], own=wcq)
            self.dma(self.gq, [(wckv.t[:, :, :], self.Wf["c_kv_up", l].rearrange("(kc p) n -> p kc n", p=128))],
                     reads=[self.dbuf(("W", "c_kv_up", l))], writes=[wckv], own=wckv)
            rotA = self.rot.t[:, 0:128]
            rotC = self.rot.t[0:64, 128:192]
            heads = ([(A_Q0 + h * 128, "A", gb + 64, self.qA[h], "qA") for h in range(8)] +
                     [(A_K0 + h * 128, "A", gb + 65, self.kA[h], "kA") for h in range(2)] +
                     [(B_Q0 + h * 128, None, gb + 66, self.qB[h], "qB") for h in range(12)] +
                     [(B_K0 + h * 128, None, gb + 67, self.kB[h], "kB") for h in range(12)])
            for t0 in range(0, S, T):
                self.norm_tile(self.xTv, "xT", t0, T, [list(range(16))], gb + 16, [2048], hT, xring, sqring, rstd)
                self.dma(self.sp, [(tabA.t[:, 0, :], self.ropeA_d[0, :, t0:t0 + T]),
                                   (tabA.t[:, 1, :], self.ropeA_d[1, :, t0:t0 + T])], writes=[tabA], own=tabA)
                self.dma(self.sp, [(tabC.t[:, 0, :], self.ropeC_d[0, :, t0:t0 + T]),
                                   (tabC.t[:, 1, :], self.ropeC_d[1, :, t0:t0 + T])], writes=[tabC], own=tabC)

                def proj_chunk(wt, c0, M):
                    banks = [self.psb() for _ in range(ntb)]
                    for kc in range(16):
                        for tb in range(ntb):
                            self.mm(banks[tb], banks[tb].t[0:M, :], wt, wt.t[:, kc, c0:c0 + M],
                                    hT[kc], hT[kc].t[:, tb * 512:(tb + 1) * 512],
                                    start=(kc == 0), stop=(kc == 15), signal=(kc == 15 and tb == ntb - 1))
                    return banks

                for hg in range(len(heads) // 2):
                    wt = wring.next()
                    col0 = heads[hg * 2][0]
                    self.dma(self.gq, [(wt.t[:, :, :], win_v[:, :, col0:col0 + 256])], reads=[win_b], writes=[wt], own=wt)
                    for hh in range(2):
                        (_, kind, gcol, dst, key) = heads[hg * 2 + hh]
                        banks = proj_chunk(wt, hh * 128, 128)
                        for tb in range(ntb):
                            tcols = slice(tb * 512, (tb + 1) * 512)
                            dcols = slice(t0 + tb * 512, t0 + (tb + 1) * 512)
                            rs = self.headnorm([(banks[tb], banks[tb].t[:, :], 128)], 128, R["sq"], self.psb(),
                                               R["rs"].next())
                            if kind == "A":
                                qn = R["qn"].next()
                                self.scale_part(banks[tb], banks[tb].t[:, :], 128, rs, gcol, qn, qn.t[:, :])
                                ob = self.rope_part(qn, 128, tabA, tcols, rotA, R)
                            else:
                                ob = R["ob"].next()
                                self.scale_part(banks[tb], banks[tb].t[:, :], 128, rs, gcol, ob, ob.t[:, :])
                            self.store_fm(ob, 128, dst[:, dcols], key)
                for (c0, ncol, dst, dc0, key) in ([(A_V0, 256, self.vA, 0, "vA")] +
                                                  [(B_V0 + i * 256, 256, self.vB, i * 256, "vB") for i in range(6)]):
                    wv = wring.next()
                    self.dma(self.gq, [(wv.t[:, :, 0:ncol], win_v[:, :, c0:c0 + ncol])], reads=[win_b], writes=[wv], own=wv)
                    for ts in range(T // 128):
                        pb = self.psb()
                        for kc in range(16):
                            self.mm(pb, pb.t[:, 0:ncol], hT[kc], hT[kc].t[:, ts * 128:(ts + 1) * 128],
                                    wv, wv.t[:, kc, 0:ncol], start=(kc == 0), stop=(kc == 15), signal=(kc == 15))
                        vo = vst.next()
                        self.op(self.act, lambda h: h.activation(out=vo.t[:, 0:ncol], in_=pb.t[:, 0:ncol], func=AF.Copy),
                                reads=[pb], writes=[vo])
                        self.dma(self.sp, [(dst[t0 + ts * 128:t0 + (ts + 1) * 128, dc0:dc0 + ncol], vo.t[:, 0:ncol])],
                                 reads=[vo], writes=[self.dbuf(key)], own=vo)
                for (c0, nch, dsts, M) in [(C_Q0, 2, zq[0:2], 128), (C_Q0 + 256, 2, zq[2:4], 128),
                                           (C_KV0, 2, zkv, 128), (C_R0, 1, [kr], 64)]:
                    wt = wring.next()
                    ncol = 256 if M == 128 else 64
                    self.dma(self.gq, [(wt.t[:, :, 0:ncol], win_v[:, :, c0:c0 + ncol])], reads=[win_b], writes=[wt], own=wt)
                    for ci in range(nch):
                        banks = proj_chunk(wt, ci * 128, M)
                        for tb in range(ntb):
                            d = dsts[ci]
                            self.op(self.act, lambda h: h.activation(out=d.t[0:M, tb * 512:(tb + 1) * 512],
                                                                     in_=banks[tb].t[0:M, :], func=AF.Copy),
                                    reads=[banks[tb]], writes=[d])
                for tb in range(ntb):
                    tcols = slice(tb * 512, (tb + 1) * 512)
                    dcols = slice(t0 + tb * 512, t0 + (tb + 1) * 512)
                    rs = self.headnorm([(zq[i], zq[i].t[:, tcols], 128) for i in range(4)], 512, R["sq"], self.psb(),
                                       R["rs"].next())
                    for i in range(4):
                        self.scale_part(zq[i], zq[i].t[:, tcols], 128, rs, gb + 68 + i, cqn[i], cqn[i].t[:, tcols])
                    rs = self.headnorm([(zkv[i], zkv[i].t[:, tcols], 128) for i in range(2)], 256, R["sq"], self.psb(),
                                       R["rs"].next())
                    for i in range(2):
                        self.scale_part(zkv[i], zkv[i].t[:, tcols], 128, rs, gb + 72 + i, ckvn[i], ckvn[i].t[:, tcols])
                    for hC in range(4):
                        pn, pr = self.psb(), self.psb()
                        for kc in range(4):
                            self.mm(pn, pn.t[:, :], wcq, wcq.t[:, kc, hC * 192:hC * 192 + 128], cqn[kc], cqn[kc].t[:, tcols],
                                    start=(kc == 0), stop=(kc == 3), signal=False)
                        for kc in range(4):
                            self.mm(pr, pr.t[0:64, :], wcq, wcq.t[:, kc, hC * 192 + 128:hC * 192 + 192], cqn[kc],
                                    cqn[kc].t[:, tcols], start=(kc == 0), stop=(kc == 3), signal=(kc == 3))
                        rs = self.headnorm([(pn, pn.t[:, :], 128), (pr, pr.t[0:64, :], 64)], 192, R["sq"], self.psb(),
                                           R["rs"].next())
                        ob = R["ob"].next()
                        self.scale_part(pn, pn.t[:, :], 128, rs, gb + 74, ob, ob.t[:, :])
                        self.store_fm(ob, 128, self.qCn[hC][:, dcols], "qCn")
                        qn = R["qn"].next()
                        self.scale_part(pr, pr.t[0:64, :], 64, rs, gb + 75, qn, qn.t[0:64, :])
                        ob = self.rope_part(qn, 64, tabC, tcols, rotC, R)
                        self.store_fm(ob, 64, self.qCr[hC][:, dcols], "qCr")
                        pk = self.psb()
                        for kc in range(2):
                            self.mm(pk, pk.t[:, :], wckv, wckv.t[:, kc, hC * 256:hC * 256 + 128], ckvn[kc],
                                    ckvn[kc].t[:, tcols], start=(kc == 0), stop=(kc == 1), signal=(kc == 1))
                        rs = self.headnorm([(pk, pk.t[:, :], 128), (kr, kr.t[0:64, tcols], 64)], 192, R["sq"], self.psb(),
                                           R["rs"].next())
                        ob = R["ob"].next()
                        self.scale_part(pk, pk.t[:, :], 128, rs, gb + 76, ob, ob.t[:, :])
                        self.store_fm(ob, 128, self.kCn[hC][:, dcols], "kCn")
                        qn = R["qn"].next()
                        self.scale_part(kr, kr.t[0:64, tcols], 64, rs, gb + 77, qn, qn.t[0:64, :])
                        ob = self.rope_part(qn, 64, tabC, tcols, rotC, R)
                        self.store_fm(ob, 64, self.kCr[hC][:, dcols], "kCr")
                wv_view = wckv.t[:, :, :].rearrange("p k (h two d) -> p k h two d", two=2, d=128)
                for ts in range(T // 128):
                    pb = self.psb()
                    for kc in range(2):
                        self.mm(pb, pb.t[:, :].rearrange("p (h d) -> p h d", d=128), ckvn[kc],
                                ckvn[kc].t[:, ts * 128:(ts + 1) * 128], wckv, wv_view[:, kc, :, 1, :],
                                start=(kc == 0), stop=(kc == 1), signal=(kc == 1))
                    vo = vst.next()
                    self.op(self.act, lambda h: h.activation(out=vo.t[:, :], in_=pb.t[:, :], func=AF.Copy),
                            reads=[pb], writes=[vo])
                    self.dma(self.sp, [(self.vC[t0 + ts * 128:t0 + (ts + 1) * 128, :], vo.t[:, :])],
                             reads=[vo], writes=[self.dbuf("vC")], own=vo)

    def attend(self, qparts, kparts, V, scale, chunk, pring, oring, rdring, pbank):
        S = self.S
        nkc = S // 128
        banks = self.psring.bufs
        for qb in range(S // 512):
            Ob, Db = banks[4 + 2 * (pbank[0] % 2)], banks[5 + 2 * (pbank[0] % 2)]
            pbank[0] += 1
            qs = slice(qb * 512, (qb + 1) * 512)

            def emitS(kc):
                sbk = banks[pbank[1] % 4]
                pbank[1] += 1
                n = len(qparts)
                for pi in range(n):
                    (qb_, P) = qparts[pi]
                    (kb_, _) = kparts[pi]
                    self.mm(sbk, sbk.t[:, :], kb_, kb_.t[0:P, kc * 128:(kc + 1) * 128], qb_, qb_.t[0:P, qs],
                            start=(pi == 0), stop=(pi == n - 1), signal=(pi == n - 1))
                return sbk
            nxt = emitS(0)
            for kc in range(nkc):
                cur = nxt
                if kc + 1 < nkc:
                    nxt = emitS(kc + 1)
                P_ = pring.next()
                self.op(self.act, lambda h: h.activation(out=P_.t[:, :], in_=cur.t[:, :], func=AF.Exp, scale=float(scale)),
                        reads=[cur], writes=[P_])
                self.mm(Ob, Ob.t[:, :], V, V.t[:, kc, :], P_, P_.t[:, :], start=(kc == 0), stop=(kc == nkc - 1),
                        signal=False)
                self.mm(Db, Db.t[:, :], self.ones, self.ones.t[:, :], P_, P_.t[:, :], start=(kc == 0),
                        stop=(kc == nkc - 1), signal=True)
            rd = rdring.next()
            self.op(self.dve, lambda h: h.reciprocal(out=rd.t[:, :], in_=Db.t[:, :]), reads=[Db], writes=[rd])
            o = oring.next()
            self.op(self.dve, lambda h: h.tensor_tensor(out=o.t[:, :], in0=Ob.t[:, :], in1=rd.t[:, :], op=ALU.mult),
                    reads=[Ob, rd], writes=[o])
            self.dma(self.sp, [(self.oTv[:, chunk, qs], o.t[:, :])], reads=[o],
                     writes=[self.dbuf(("oT", chunk, qb // 2))], own=o)

    def phase_m2_dense(self, l, which):
        S = self.S
        with ExitStack() as es:
            sb = lambda n, sh, dt_, dma=False: self.sb(es, n, sh, dt_, dma)
            pring = Ring([sb(f"ap{i}", [128, 512], BF16) for i in range(4)])
            oring = Ring([sb(f"ao{i}", [128, 512], F32, True) for i in range(2)])
            rdring = Ring([sb(f"ard{i}", [128, 512], F32) for i in range(2)])
            pbank = [0, 0]
            if which == "A":
                kring = Ring([sb(f"ak{i}", [128, S], BF16, True) for i in range(2)])
                vring = Ring([sb(f"av{i}", [128, S // 128, 128], BF16, True) for i in range(2)])
                qring = Ring([sb(f"aq{i}", [128, S], BF16, True) for i in range(2)])
                for kv in range(2):
                    kT, V = kring.next(), vring.next()
                    self.dma(self.sp, [(kT.t[:, :], self.kA[kv])], reads=[self.dbuf("kA")], writes=[kT], own=kT)
                    self.dma(self.sp, [(V.t[:, :, :], self.vA.rearrange("(c p) f -> p c f", p=128)[:, :, kv * 128:(kv + 1) * 128])],
                             reads=[self.dbuf("vA")], writes=[V], own=V)
                    for g in range(4):
                        hq = kv * 4 + g
                        q = qring.next()
                        self.dma(self.sp, [(q.t[:, :], self.qA[hq])], reads=[self.dbuf("qA")], writes=[q], own=q)
                        self.attend([(q, 128)], [(kT, 128)], V, 128 ** -0.5, hq, pring, oring, rdring, pbank)
            else:
                knr = Ring([sb(f"ckn{i}", [128, S], BF16, True) for i in range(2)])
                krr = Ring([sb(f"ckr{i}", [64, S], BF16, True) for i in range(2)])
                vring = Ring([sb(f"cv{i}", [128, S // 128, 128], BF16, True) for i in range(2)])
                qnr = Ring([sb(f"cqn{i}", [128, S], BF16, True) for i in range(2)])
                qrr = Ring([sb(f"cqr{i}", [64, S], BF16, True) for i in range(2)])
                for hC in range(4):
                    kn, krp, V, qn, qr = knr.next(), krr.next(), vring.next(), qnr.next(), qrr.next()
                    self.dma(self.sp, [(kn.t[:, :], self.kCn[hC])], reads=[self.dbuf("kCn")], writes=[kn], own=kn)
                    self.dma(self.sp, [(krp.t[:, :], self.kCr[hC])], reads=[self.dbuf("kCr")], writes=[krp], own=krp)
                    self.dma(self.sp, [(V.t[:, :, :], self.vC.rearrange("(c p) f -> p c f", p=128)[:, :, hC * 128:(hC + 1) * 128])],
                             reads=[self.dbuf("vC")], writes=[V], own=V)
                    self.dma(self.sp, [(qn.t[:, :], self.qCn[hC])], reads=[self.dbuf("qCn")], writes=[qn], own=qn)
                    self.dma(self.sp, [(qr.t[:, :], self.qCr[hC])], reads=[self.dbuf("qCr")], writes=[qr], own=qr)
                    self.attend([(qn, 128), (qr, 64)], [(kn, 128), (krp, 64)], V, 192 ** -0.5, 12 + hC,
                                pring, oring, rdring, pbank)

    def phase_m2_b(self, l):
        S = self.S
        with ExitStack() as es:
            sb = lambda n, sh, dt_, dma=False: self.sb(es, n, sh, dt_, dma)
            E = sb("bE", [128, 12, 256], F32, True)
            self.dma(self.sp, [(E.t[:, :, :], self.alibi_d[:, :, :])], writes=[E], own=E)
            qring = Ring([sb(f"bq{i}", [128, S], BF16, True) for i in range(2)])
            kring = Ring([sb(f"bk{i}", [128, S], BF16, True) for i in range(2)])
            vring = Ring([sb(f"bv{i}", [128, S // 128, 128], BF16, True) for i in range(2)])
            accs = Ring([sb(f"bacc{i}", [128, 2, S], F32, True) for i in range(2)])
            pering = Ring([sb(f"bpe{i}", [128, 256], F32) for i in range(3)])
            pmring = Ring([sb(f"bpm{i}", [128, 256], BF16) for i in range(3)])
            for s_ in range(4):
                acc = accs.next()
                self.op(self.pool, lambda h: h.memset(acc.t[:, :, :], 0.0), writes=[acc])
                for g, (win, dil) in enumerate(B_CONFIGS):
                    hd = g * 4 + s_
                    Lr = S // dil
                    nch = Lr // 128
                    q, k, V = qring.next(), kring.next(), vring.next()
                    self.dma(self.sp, [(q.t[:, :], self.qB[hd])], reads=[self.dbuf("qB")], writes=[q], own=q)
                    self.dma(self.sp, [(k.t[:, :], self.kB[hd])], reads=[self.dbuf("kB")], writes=[k], own=k)
                    vsrc = self.vB.rearrange("(c p r) f -> p r c f", p=128, r=dil)
                    pairs = []
                    for r in range(dil):
                        pairs.append((V.t[:, r * nch:(r + 1) * nch, :], vsrc[:, r, :, hd * 128:(hd + 1) * 128]))
                    self.dma(self.sp, pairs, reads=[self.dbuf("vB")], writes=[V], own=V)
                    for r in range(dil):
                        for kc in range(nch):
                            ka = kc * 128
                            j0 = 64 if kc == 0 else 0
                            j1 = 192 if kc == nch - 1 else 256
                            n = j1 - j0
                            lq0 = ka - 64 + j0
                            qsl = slice(r + dil * lq0, r + dil * (lq0 + n - 1) + 1, dil)
                            ksl = slice(r + dil * ka, r + dil * (ka + 127) + 1, dil)
                            sbk = self.psb()
                            self.mm(sbk, sbk.t[:, 0:n], k, k.t[:, ksl], q, q.t[:, qsl], start=True, stop=True, signal=True)
                            pe_ = pering.next()
                            self.op(self.act, lambda h: h.activation(out=pe_.t[:, 0:n], in_=sbk.t[:, 0:n], func=AF.Exp,
                                                                     scale=float(128 ** -0.5)), reads=[sbk], writes=[pe_])
                            pm = pmring.next()
                            self.op(self.dve, lambda h: h.tensor_tensor(out=pm.t[:, 0:n], in0=pe_.t[:, 0:n],
                                                                        in1=E.t[:, hd, j0:j1], op=ALU.mult),
                                    reads=[pe_, E], writes=[pm])
                            ob = self.psb()
                            self.mm(ob, ob.t[:, 0:n], V, V.t[:, r * nch + kc, :], pm, pm.t[:, 0:n], start=True, stop=True,
                                    signal=False)
                            self.mm(ob, ob.t[:, 256:256 + n], self.ones, self.ones.t[:, :], pm, pm.t[:, 0:n], start=False,
                                    stop=True, signal=True, skip_group_check=True)
                            self.op(self.dve, lambda h: h.tensor_tensor(
                                out=acc.t[:, :, qsl], in0=acc.t[:, :, qsl],
                                in1=ob.t[:, :].rearrange("p (two m) -> p two m", two=2)[:, :, 0:n], op=ALU.add),
                                reads=[acc, ob], writes=[acc])
                self.op(self.dve, lambda h: h.reciprocal(out=acc.t[:, 1, :], in_=acc.t[:, 1, :]), reads=[acc], writes=[acc])
                self.op(self.dve, lambda h: h.tensor_tensor(out=acc.t[:, 0, :], in0=acc.t[:, 0, :], in1=acc.t[:, 1, :],
                                                            op=ALU.mult), reads=[acc], writes=[acc])
                self.dma(self.sp, [(self.oTv[:, 8 + s_, :], acc.t[:, 0, :])], reads=[acc],
                         writes=[self.dbuf(("oT", 8 + s_, ti)) for ti in range(max(1, S // 1024))], own=acc)

    def phase_m3(self, l):
        S = self.S
        T = 1024
        gb = l * GL
        wo_v = self.Wf["w_out", l].rearrange("(k p) n -> p k n", p=128)
        wo_b = self.dbuf(("W", "w_out", l))
        with ExitStack() as es:
            sb = lambda n, sh, dt_, dma=False: self.sb(es, n, sh, dt_, dma)
            xring = Ring([sb(f"ox{i}", [128, T], F32, True) for i in range(3)])
            sqring = Ring([sb(f"osq{i}", [128, T], BF16) for i in range(2)])
            rstd = [sb(f"orstd{i}", [128, T], F32) for i in range(3)]
            yT = [sb(f"oy{c}", [128, T], BF16) for c in range(16)]
            wdring = Ring([sb(f"owd{i}", [128, 8, 256], BF16, "g") for i in range(3)])
            oring = Ring([sb(f"oo{i}", [128, 512], F32, True) for i in range(3)])
            for t0 in range(0, S, T):
                self.norm_tile(self.oTv, "oT", t0, T, [list(range(8)), list(range(8, 12)), list(range(12, 16))],
                               gb + 32, [1024, 512, 512], yT, xring, sqring, rstd)
                self.proj_resid(wo_v, wo_b, 16, 8, yT, t0, T, 1.0, wdring, xring, oring)

def _bf16(a):
    return np.asarray(a, dtype=np.float32).astype(ml_dtypes.bfloat16)


def make_consts():
    c = {}
    c["ident"] = np.eye(128, dtype=np.float32)
    rot = np.zeros((128, 192), np.float32)
    for f in range(128):
        blk, o = divmod(f, 64)
        if o < 32:
            rot[blk * 64 + o + 32, f] = -1.0
        else:
            rot[blk * 64 + o - 32, f] = 1.0
    for f in range(64):
        if f < 32:
            rot[f + 32, 128 + f] = -1.0
        else:
            rot[f - 32, 128 + f] = 1.0
    c["rotm"] = _bf16(rot)
    t = np.arange(4096, dtype=np.float32)
    row = np.floor(t / 64.0).astype(np.float32)
    col = (t - row * 64.0).astype(np.float32)
    inv32 = (10000.0 ** (-np.arange(32, dtype=np.float32) / 32.0)).astype(np.float32)
    ropeA = np.zeros((2, 128, 4096), np.float32)
    for f in range(128):
        blk, o = divmod(f, 64)
        pos = row if blk == 0 else col
        ang = (pos * inv32[o % 32]).astype(np.float32)
        ropeA[0, f] = np.cos(ang)
        ropeA[1, f] = np.sin(ang)
    c["ropeA"] = ropeA
    ropeC = np.zeros((2, 64, 4096), np.float32)
    for f in range(64):
        ang = (t * inv32[f % 32]).astype(np.float32)
        ropeC[0, f] = np.cos(ang)
        ropeC[1, f] = np.sin(ang)
    c["ropeC"] = ropeC
    slopes = (2.0 ** (-8.0 * np.arange(1, 13, dtype=np.float32) / 12.0)).astype(np.float32)
    k = np.arange(128)[:, None]
    j = np.arange(256)[None, :]
    d = np.abs(k - j + 64)
    E = np.zeros((128, 12, 256), np.float32)
    for hd in range(12):
        dil = B_CONFIGS[hd // 4][1]
        E[:, hd, :] = np.where(d <= 64, np.exp(-slopes[hd] * d.astype(np.float32) * dil), 0.0)
    c["alibiE"] = E.astype(np.float32)
    return c


def make_gains(inp, L):
    g = np.zeros((128, L * GL), np.float32)

    def put(col, vec):
        v = np.asarray(vec, np.float32).reshape(-1, 128)
        g[:, col:col + v.shape[0]] = v.T

    for l in range(L):
        b = l * GL
        put(b + 0, inp["ffn1_norm"][l])
        put(b + 16, inp["mix_norm"][l])
        put(b + 32, inp["out_norm"][l])
        put(b + 48, inp["ffn2_norm"][l])
        put(b + 64, inp["a_q_norm"][l])
        put(b + 65, inp["a_k_norm"][l])
        put(b + 66, inp["b_q_norm"][l])
        put(b + 67, inp["b_k_norm"][l])
        put(b + 68, inp["c_q_a_norm"][l])
        put(b + 72, inp["c_kv_a_norm"][l])
        for nm, col in (("c_q_norm", 74), ("c_k_norm", 76)):
            v = np.asarray(inp[nm][l], np.float32)
            g[:, b + col] = v[:128]
            g[:64, b + col + 1] = v[128:192]
    return g


WEIGHT_NAMES = ["ffn1_w_gu", "ffn1_w_down", "w_in", "c_q_up", "c_kv_up", "w_out", "ffn2_w_gu", "ffn2_w_down"]


def make_in_maps(inp, n_cores, L=2, S=4096, wnames=None, nranks=None):
    wnames = WEIGHT_NAMES if wnames is None else wnames
    nranks = n_cores if nranks is None else nranks
    consts = make_consts()
    gains = make_gains(inp, 2)
    x = np.asarray(inp["x"], np.float32)
    maps = []
    for i in range(n_cores):
        m = dict(consts)
        m["gains"] = gains
        for nm in wnames:
            w = np.asarray(inp[nm], np.float32)[:L]
            rs = w.shape[1] // nranks
            m[nm] = np.ascontiguousarray(w[:, i * rs:(i + 1) * rs, :]) if nranks > 1 else np.ascontiguousarray(w)
        m["x"] = np.ascontiguousarray(x[i, :S])
        maps.append(m)
    return maps


def kernel(**inputs):
    kb = KB(S=4096, L=2)
    nc = kb.build()
    maps = make_in_maps(inputs, NCORES)
    res = run_bass_kernel_spmd(nc, maps, core_ids=list(range(NCORES)))
    out = np.stack([np.asarray(r["out"], np.float32) for r in res.results], axis=0)
    return out
```
